# Optimizing a Trainium2 kernel written in Bass

```python
import jax, jax.numpy as jnp
from jax import lax
import numpy as np

D_MODEL = 1024
BATCH = 8
SEQ = 2048
DEPTH = 2

POOL_WINDOWS = (2, 4, 8, 16)
POOL_GROUPS = len(POOL_WINDOWS)
POOL_GROUP_DIM = D_MODEL // 8
POOL_WIDTH = POOL_GROUPS * POOL_GROUP_DIM
HGRN_HEAD_DIM = 128
HGRN_HEADS = D_MODEL // HGRN_HEAD_DIM
HGRN_WIDTH = HGRN_HEADS * HGRN_HEAD_DIM
CHUNK = 64
NORM_EPS = 1e-6
IN_SIZES = (POOL_WIDTH, POOL_WIDTH, HGRN_WIDTH, HGRN_WIDTH, HGRN_WIDTH, HGRN_WIDTH, D_MODEL, D_MODEL)
IN_WIDTH = sum(IN_SIZES)

kernel_name = "hybrid_pool_hgrn2_gated_block"


def rms_norm(x, g):
    xf = x.astype(jnp.float32)
    y = xf * lax.rsqrt(jnp.mean(xf * xf, axis=-1, keepdims=True) + NORM_EPS)
    return (y * g.astype(jnp.float32)).astype(x.dtype)


def multiscale_pool(u):
    b, s, _ = u.shape
    uf = u.astype(jnp.float32)
    csum = lax.cumsum(uf, axis=1)
    pos = jnp.arange(s, dtype=jnp.float32) + 1.0
    outs = []
    for gi, w in enumerate(POOL_WINDOWS):
        sl = slice(gi * POOL_GROUP_DIM, (gi + 1) * POOL_GROUP_DIM)
        cg = csum[:, :, sl]
        prev = jnp.pad(cg, ((0, 0), (w, 0), (0, 0)))[:, :s]
        count = jnp.minimum(pos, float(w))[None, :, None]
        outs.append((cg - prev) / count - uf[:, :, sl])
    return jnp.stack(outs, axis=2)


def _hgrn2_chunk_step(state, inp):
    q, k, v, logf = inp
    cum = jnp.cumsum(logf, axis=2)
    o_inter = jnp.einsum('bhtk,bhkv->bhtv', q * jnp.exp(cum), state)
    c = q.shape[2]
    causal = jnp.tril(jnp.ones((c, c), dtype=bool))[:, :, None]
    diff = cum[:, :, :, None, :] - cum[:, :, None, :, :]
    decay = jnp.where(causal, jnp.exp(jnp.minimum(diff, 0.0)), 0.0)
    scores = jnp.sum(q[:, :, :, None, :] * k[:, :, None, :, :] * decay, axis=-1)
    o = o_inter + jnp.einsum('bhts,bhsv->bhtv', scores, v)
    last = cum[:, :, -1:, :]
    new_state = (jnp.exp(last[:, :, 0, :])[..., None] * state
                 + jnp.einsum('bhsk,bhsv->bhkv', k * jnp.exp(last - cum), v))
    return new_state, o


def hgrn2(q, k, v, logf):
    b, s, h, dk = q.shape
    n = s // CHUNK

    def to_chunks(t):
        return t.reshape(b, n, CHUNK, h, t.shape[-1]).transpose(1, 0, 3, 2, 4)

    state0 = jnp.zeros((b, h, dk, v.shape[-1]), jnp.float32)
    _, o = lax.scan(_hgrn2_chunk_step, state0,
                    (to_chunks(q), to_chunks(k), to_chunks(v), to_chunks(logf)))
    return o.transpose(1, 0, 3, 2, 4).reshape(b, s, h, v.shape[-1])


def setup_inputs(seed: int = 0) -> dict:
    key = jax.random.key(seed)
    ks = jax.random.split(key, 16)
    L, D = DEPTH, D_MODEL
    nrm = jax.random.normal
    return {
        "x": nrm(ks[0], (BATCH, SEQ, D), jnp.float32),
        "c": nrm(ks[1], (BATCH, D), jnp.float32),
        "w_ada": nrm(ks[2], (L, D, 3 * D), jnp.float32) * (0.5 * D ** -0.5),
        "b_ada": nrm(ks[3], (L, 3 * D), jnp.float32) * 0.02,
        "g_pre": 1.0 + 0.02 * nrm(ks[4], (L, D), jnp.float32),
        "g_post": 1.0 + 0.02 * nrm(ks[5], (L, D), jnp.float32),
        "w_in": nrm(ks[6], (L, D, IN_WIDTH), jnp.float32) * D ** -0.5,
        "pool_w": nrm(ks[7], (L, POOL_GROUPS, POOL_GROUP_DIM, POOL_GROUP_DIM), jnp.float32) * POOL_GROUP_DIM ** -0.5,
        "pool_scale": 1.0 + 0.02 * nrm(ks[8], (L, POOL_WIDTH), jnp.float32),
        "lb_logits": nrm(ks[9], (L, HGRN_WIDTH), jnp.float32),
        "hgrn_norm_g": 1.0 + 0.02 * nrm(ks[10], (L, HGRN_HEAD_DIM), jnp.float32),
        "w_pool_o": nrm(ks[11], (L, POOL_WIDTH, D), jnp.float32) * POOL_WIDTH ** -0.5,
        "w_hgrn_o": nrm(ks[12], (L, HGRN_WIDTH, D), jnp.float32) * HGRN_WIDTH ** -0.5,
        "w_out": nrm(ks[13], (L, D, D), jnp.float32) * D ** -0.5,
    }


def reference(x, c, w_ada, b_ada, g_pre, g_post, w_in, pool_w, pool_scale, lb_logits,
              hgrn_norm_g, w_pool_o, w_hgrn_o, w_out):
    b, s, d = x.shape
    p = jax.nn.softmax(lb_logits.astype(jnp.float32), axis=0)
    lower_bounds = jnp.cumsum(p, axis=0) - p[0:1]
    split_idx = np.cumsum(IN_SIZES)[:-1].tolist()
    c_act = jax.nn.silu(c)
    for l in range(DEPTH):
        ada = c_act @ w_ada[l] + b_ada[l]
        shift, scale, gate = jnp.split(ada[:, None, :], 3, axis=-1)
        h = rms_norm(x, g_pre[l]) * (1.0 + scale) + shift
        (pv, pg, hq, hf, hi, hg, mg_pool, mg_hgrn) = jnp.split(h @ w_in[l], split_idx, axis=-1)

        pooled = multiscale_pool(pv)
        pooled = jnp.einsum('bsgc,gcd->bsgd', pooled, pool_w[l].astype(jnp.float32))
        pooled = pooled.reshape(b, s, POOL_WIDTH) * pool_scale[l]
        branch_a = (pooled.astype(x.dtype) * jax.nn.silu(pg)) @ w_pool_o[l]

        shp = (b, s, HGRN_HEADS, HGRN_HEAD_DIM)
        lb = jnp.clip(lower_bounds[l], 0.0, 1.0).reshape(HGRN_HEADS, HGRN_HEAD_DIM)
        zf = hf.astype(jnp.float32).reshape(shp)
        f = lb + (1.0 - lb) * jax.nn.sigmoid(zf)
        logf = jnp.log(jnp.maximum(f, 1e-30))
        k = 1.0 - f
        q = jax.nn.silu(hq.astype(jnp.float32)).reshape(shp)
        v = hi.astype(jnp.float32).reshape(shp)
        o = hgrn2(q, k, v, logf)
        o = rms_norm(o, hgrn_norm_g[l]).astype(x.dtype).reshape(b, s, HGRN_WIDTH)
        branch_b = (o * jax.nn.silu(hg)) @ w_hgrn_o[l]

        merged = jax.nn.sigmoid(mg_pool) * branch_a + jax.nn.sigmoid(mg_hgrn) * branch_b
        y = merged @ w_out[l]
        x = x + gate * rms_norm(y, g_post[l])
    return x
```

```python
import numpy as np
import ml_dtypes
from contextlib import ExitStack
import concourse.bass as bass
import concourse.mybir as mybir
from concourse.bass_utils import run_bass_kernel_spmd

F32 = mybir.dt.float32
BF16 = mybir.dt.bfloat16
AF = mybir.ActivationFunctionType
ALU = mybir.AluOpType

D = 1024
L = 2
NH = 8
EPS = 1e-6
UNITS_PER_LAYER = 26
U_ADA, U_PV, U_PG, U_HEAD, U_D, U_OUT = 0, 6, 7, 8, 16, 24
PRM_L = 56
NPRM = 8 + L * PRM_L
NSLOT = 4


class View:
    __slots__ = ("buf", "ap", "lo", "hi")

    def __init__(self, buf, ap, lo, hi):
        self.buf, self.ap, self.lo, self.hi = buf, ap, lo, hi


class Buf:
    def __init__(self, name, handle, shape, space):
        self.name, self.h, self.shape, self.space = name, handle, list(shape), space
        fs = self.shape if space == "dram" else self.shape[1:]
        st, acc = [], 1
        for n in reversed(fs):
            st.append(acc)
            acc *= n
        self.fstr = list(reversed(st))
        self.fsize = acc
        self.acc = []
        self.excl = space == "psum"

    def __getitem__(self, idx):
        if not isinstance(idx, tuple):
            idx = (idx,)
        idx = tuple(idx) + (slice(None),) * (len(self.shape) - len(idx))
        fidx = idx if self.space == "dram" else idx[1:]
        fsh = self.shape if self.space == "dram" else self.shape[1:]
        lo = hi = 0
        for k, (i, n) in enumerate(zip(fidx, fsh)):
            if isinstance(i, slice):
                a = 0 if i.start is None else i.start
                b = n if i.stop is None else i.stop
            else:
                a, b = i, i + 1
            lo += a * self.fstr[k]
            hi += (b - 1) * self.fstr[k]
        return View(self, self.h[idx], lo, hi + 1)

    def all(self):
        return self[tuple(slice(None) for _ in self.shape)]


class _Op:
    __slots__ = ("eng", "fn", "waits", "res", "seq", "is_dma")


class Sched:
    COMPUTE = ("pe", "act", "dve", "pool")

    def __init__(self, nc):
        self.nc = nc
        self.ops = {e: [] for e in ("pe", "act", "dve", "pool", "sp")}
        self.count = {}
        self.known = {e: {} for e in self.ops}
        self.snaps = {}
        self.signalled = {}
        self.chan_names = []
        self.cap = None

    def replay(self, *lists):
        keyed = []
        for li, lst in enumerate(lists):
            keyed += [((k + 0.5) / len(lst), li, k, o) for k, o in enumerate(lst)]
        keyed.sort(key=lambda t: (t[0], t[1], t[2]))
        for _, _, _, (eng, fn, reads, writes) in keyed:
            self.op(eng, fn, reads, writes)

    def _deps(self, eng, reads, writes):
        deps = {}

        def add(r, s, raw):
            if r == eng and eng == "pe":
                return
            if deps.get(r, -1) < s:
                deps[r] = s

        for v in reads:
            for (r, s, lo, hi, w) in v.buf.acc:
                if v.buf.excl:
                    add(r, s, False)
                elif w and lo < v.hi and v.lo < hi:
                    add(r, s, True)
        for v in writes:
            for (r, s, lo, hi, w) in v.buf.acc:
                if v.buf.excl or (lo < v.hi and v.lo < hi):
                    add(r, s, False)
        return deps

    def _log(self, res, seq, reads, writes):
        for v in reads:
            if v.buf.excl:
                v.buf.acc = [(res, seq, 0, v.buf.fsize, True)]
            else:
                v.buf.acc.append((res, seq, v.lo, v.hi, False))
        for v in writes:
            b = v.buf
            if b.excl:
                b.acc = [(res, seq, 0, b.fsize, True)]
                continue
            b.acc = [a for a in b.acc if not (a[2] >= v.lo and a[3] <= v.hi)]
            b.acc.append((res, seq, v.lo, v.hi, True))

    def _resolve(self, eng, deps):
        kn = self.known[eng]
        waits = []
        for r, s in sorted(deps.items()):
            if kn.get(r, -1) >= s:
                continue
            waits.append((r, s))
            self.signalled.setdefault(r, set()).add(s)
        for r, s in waits:
            for k2, v2 in self.snaps[(r, s)].items():
                if kn.get(k2, -1) < v2:
                    kn[k2] = v2
            if kn.get(r, -1) < s:
                kn[r] = s
        return waits

    def op(self, eng, fn, reads=(), writes=()):
        if self.cap is not None:
            self.cap.append((eng, fn, list(reads), list(writes)))
            return None
        deps = self._deps(eng, reads, writes)
        o = _Op()
        o.eng, o.fn, o.res, o.is_dma = eng, fn, eng, False
        o.waits = self._resolve(eng, deps)
        o.seq = self.count.get(eng, 0)
        self.count[eng] = o.seq + 1
        self.snaps[(eng, o.seq)] = dict(self.known[eng])
        self._log(eng, o.seq, reads, writes)
        self.ops[eng].append(o)
        return o

    def dma(self, q, chan, out, in_, **kw):
        if chan not in self.chan_names:
            self.chan_names.append(chan)
        reads, writes = [in_], [out]
        deps = self._deps(q, reads, writes)
        seq = self.count.get(chan, 0)
        if seq > 0:
            deps[chan] = max(deps.get(chan, -1), seq - 1)
        o = _Op()
        o.eng, o.res, o.is_dma, o.seq = q, chan, True, seq
        oa, ia = out.ap, in_.ap
        o.fn = lambda e, oa=oa, ia=ia, kw=kw: e.dma_start(out=oa, in_=ia, **kw)
        o.waits = self._resolve(q, deps)
        self.count[chan] = seq + 1
        self.snaps[(chan, seq)] = dict(self.known[q])
        self._log(chan, seq, reads, writes)
        self.ops[q].append(o)
        return o

    def wait_all(self, eng, chans):
        deps = {c: self.count[c] - 1 for c in chans if self.count.get(c, 0) > 0}
        o = _Op()
        o.eng, o.fn, o.res, o.is_dma, o.seq = eng, None, None, False, -1
        o.waits = self._resolve(eng, deps)
        self.ops[eng].append(o)

    def emit(self, stack):
        nc = self.nc
        sems = {}
        for r in list(self.COMPUTE) + self.chan_names:
            sems[r] = stack.enter_context(nc.semaphore("s_" + r))
        cum = {}
        for r in self.COMPUTE:
            sig = sorted(self.signalled.get(r, ()))
            cum[r] = {s: i + 1 for i, s in enumerate(sig)}

        def val(r, s):
            return cum[r][s] if r in self.COMPUTE else 16 * (s + 1)

        block = stack.enter_context(nc.Block())
        handles = {"pe": block.tensor, "act": block.scalar, "dve": block.vector,
                   "pool": block.gpsimd, "sp": block.sync}
        for eng, dec in handles.items():
            ops = self.ops[eng]

            def body(e, ops=ops, eng=eng):
                for o in ops:
                    for (r, s) in o.waits:
                        e.wait_ge(sems[r], val(r, s))
                    if o.fn is None:
                        continue
                    ins = o.fn(e)
                    if o.is_dma:
                        ins.then_inc(sems[o.res], 16)
                    elif o.seq in cum[eng]:
                        ins.then_inc(sems[eng], 1)

            dec(body)


def unit_schedule(nseg):
    out = []
    for l in range(L):
        b = l * UNITS_PER_LAYER
        if l == 0:
            out += [b + U_ADA + i for i in range(6)]
        for seg in range(nseg):
            out += [b + U_PV, b + U_PG]
            for h in range(NH):
                if h == NH // 2 and seg == 0 and l + 1 < L:
                    out += [(l + 1) * UNITS_PER_LAYER + U_ADA + i for i in range(6)]
                out.append(b + U_HEAD + h)
            out += [b + U_D + j for j in range(8)]
            out += [b + U_OUT, b + U_OUT + 1]
    return out


def build(S, nlayers=L):
    assert S % 256 == 0
    NTILE = S // 128
    TS = min(1024, S)
    NSEG = S // TS
    NT = TS // 128
    TBW = 256
    NCH = TBW // 128
    NTB = TS // TBW

    nc = bass.Bass("TRN2", target_bir_lowering=False)
    xd = nc.dram_tensor("x", [S, D], F32, kind="ExternalInput")
    wud = nc.dram_tensor("wu", [L * UNITS_PER_LAYER, 128, 4096], F32, kind="ExternalInput")
    prmd = nc.dram_tensor("prm", [128, NPRM], F32, kind="ExternalInput")
    cbd = nc.dram_tensor("cb", [128, 384 + 1536], BF16, kind="ExternalInput")
    pwd = nc.dram_tensor("pw", [128, L * 512], F32, kind="ExternalInput")
    outd = nc.dram_tensor("out", [S, D], F32, kind="ExternalOutput")

    with ExitStack() as st:
        def sb(name, shape, dt):
            return Buf(name, st.enter_context(nc.sbuf_tensor(name, shape, dt)), shape, "sbuf")

        def ps(name, shape, dt):
            return Buf(name, st.enter_context(nc.psum_tensor(name, shape, dt)), shape, "psum")

        Xd = Buf("x_d", xd, [S, D], "dram")
        WUd = Buf("wu_d", wud, [L * UNITS_PER_LAYER, 128, 4096], "dram")
        PRMd = Buf("prm_d", prmd, [128, NPRM], "dram")
        CBd = Buf("cb_d", cbd, [128, 1920], "dram")
        PWd = Buf("pw_d", pwd, [128, L * 512], "dram")
        OUTd = Buf("out_d", outd, [S, D], "dram")

        sc = Sched(nc)

        x = sb("xres", [128, NTILE, D], F32)
        hT = sb("hT", [128, 8, TS], BF16)
        Ain = sb("Ain", [128, 4, TS], BF16)
        Og = sb("Og", [128, 8, TS], BF16)
        mer = sb("mer", [128, 8, TS], BF16)
        slots = [sb(f"wslot{i}", [128, 4096], BF16) for i in range(NSLOT)]
        prm = sb("prm_s", [128, NPRM], F32)
        identF = sb("identF", [128, 128], F32)
        cb = sb("cb_s", [128, 1920], BF16)
        pw = sb("pw_s", [128, L * 512], BF16)
        ggt = sb("ggt", [128, D], F32)
        Sst = sb("Sst", [128, NH, 128], F32)
        onesf = sb("onesf", [128, 128], F32)
        small = sb("small", [128, 320], F32)
        OFF_FC = 1536
        NF = max(2048, OFF_FC + 16 * TBW)
        OFF_XN, OFF_PLS, OFF_C = 1024, 3072, 1024
        OFF_PB = OFF_C + 14 * TBW
        OFF_JUNK = OFF_PB + 512
        NB = OFF_JUNK + 1024
        smallb = sb("smallb", [128, 16], BF16)
        arF = sb("arF", [128, NF], F32)
        arB = sb("arB", [128, NB], BF16)

        banks = [ps(f"pb{i}", [128, 512], F32) for i in range(6)]
        bbanks = [ps(f"pbb{i}", [128, 1024], BF16) for i in range(2)]
        rot = {"f": 0, "b": 0}

        def bank():
            b = banks[rot["f"] % 6]
            rot["f"] += 1
            return b

        def bbank():
            b = bbanks[rot["b"] % 2]
            rot["b"] += 1
            return b

        ident = cb[:, 0:128]
        maskT = cb[:, 128:256]
        ones_b = cb[:, 256:384]

        def PM(i):
            return cb[:, 384 + i * 128: 384 + (i + 1) * 128]

        def _v(a):
            return isinstance(a, View)

        def ACT(out, in_, func, scale=None, bias=None, accum=None):
            kw, rd, wr = {}, [in_], [out]
            if scale is not None:
                kw["scale"] = scale.ap if _v(scale) else scale
                if _v(scale):
                    rd.append(scale)
            if bias is not None:
                kw["bias"] = bias.ap if _v(bias) else bias
                if _v(bias):
                    rd.append(bias)
            if accum is not None:
                kw["accum_out"] = accum.ap
                wr.append(accum)
            sc.op("act", lambda e: e.activation(out=out.ap, in_=in_.ap, func=func, **kw), rd, wr)

        def TS_(eng, out, in0, s1, s2, op0, op1=None):
            rd = [in0] + [s for s in (s1, s2) if _v(s)]
            a1 = s1.ap if _v(s1) else s1
            a2 = s2.ap if _v(s2) else s2
            if op1 is None:
                sc.op(eng, lambda e: e.tensor_scalar(out=out.ap, in0=in0.ap, scalar1=a1, scalar2=None, op0=op0), rd, [out])
            else:
                sc.op(eng, lambda e: e.tensor_scalar(out=out.ap, in0=in0.ap, scalar1=a1, scalar2=a2, op0=op0, op1=op1), rd, [out])

        def TT(eng, out, in0, in1, op):
            sc.op(eng, lambda e: e.tensor_tensor(out=out.ap, in0=in0.ap, in1=in1.ap, op=op), [in0, in1], [out])

        def STT(out, in0, scalar, in1, op0, op1):
            rd = [in0, in1] + ([scalar] if _v(scalar) else [])
            s_ = scalar.ap if _v(scalar) else scalar
            sc.op("dve", lambda e: e.scalar_tensor_tensor(out=out.ap, in0=in0.ap, scalar=s_, in1=in1.ap, op0=op0, op1=op1), rd, [out])

        def MM(out, lhsT, rhs, start, stop):
            sc.op("pe", lambda e: e.matmul(out=out.ap, lhsT=lhsT.ap, rhs=rhs.ap, start=start, stop=stop), [lhsT, rhs], [out])

        def TR(out, in_):
            sc.op("pe", lambda e: e.transpose(out=out.ap, in_=in_.ap, identity=ident.ap), [in_, ident], [out])

        def COPY(eng, out, in_):
            if eng == "act":
                sc.op("act", lambda e: e.copy(out=out.ap, in_=in_.ap), [in_], [out])
            else:
                sc.op(eng, lambda e: e.tensor_copy(out=out.ap, in_=in_.ap), [in_], [out])

        usched = unit_schedule(NSEG)
        ust = {"next_use": 0, "next_dma": 0}

        def _issue_unit(i):
            u = usched[i]
            slot = slots[i % NSLOT]
            src = View(WUd, wud[u].rearrange("p (a b) -> p a b", b=2048), u * 128 * 4096, (u + 1) * 128 * 4096)
            dst = View(slot, slot.h[:, :].rearrange("p (a b) -> p a b", b=2048), 0, 4096)
            sc.dma("pool", f"w{i % NSLOT}", dst, src)

        def next_unit(expect):
            i = ust["next_use"]
            assert usched[i] == expect, (i, usched[i], expect)
            while ust["next_dma"] < min(len(usched), NSLOT):
                _issue_unit(ust["next_dma"])
                ust["next_dma"] += 1
            assert i < ust["next_dma"]
            ust["next_use"] = i + 1
            return slots[i % NSLOT]

        def release(n=1):
            for _ in range(n):
                if ust["next_dma"] < len(usched):
                    _issue_unit(ust["next_dma"])
                    ust["next_dma"] += 1

        def W(slot, kc, a, b, width=512):
            return slot[:, kc * width + a: kc * width + b]

        sc.dma("sp", "prm", prm.all(), PRMd.all())
        sc.dma("sp", "prm", cb.all(), CBd.all())
        sc.dma("pool", "pwc", pw.all(), PWd.all())
        for t in range(NTILE):
            sc.dma("sp", f"x{t % 4}", x[:, t, :], Xd[t * 128:(t + 1) * 128, :])

        for bkz in banks:
            sc.op("dve", lambda e, bkz=bkz: e.memset(bkz.all().ap, 0.0), [], [bkz.all()])
        sc.op("dve", lambda e: e.memset(onesf.all().ap, 1.0), [], [onesf.all()])
        cact = small[:, 0:8]
        ACT(cact, prm[:, 0:8], AF.Silu)
        cact_b = smallb[:, 0:8]
        assert L == 2
        l0 = prm[:, 8 + 29: 8 + 37]
        l1 = prm[:, 8 + PRM_L + 29: 8 + PRM_L + 37]
        mx, e0, e1, ssum, p0, p1 = (small[:, 8 + 8 * i: 16 + 8 * i] for i in range(6))
        oml = [small[:, 64:72], small[:, 72:80]]
        lbs = [small[:, 80:88], small[:, 88:96]]
        TT("dve", mx, l0, l1, ALU.max)
        TT("dve", e0, l0, mx, ALU.subtract)
        TT("dve", e1, l1, mx, ALU.subtract)
        ACT(e0, e0, AF.Exp)
        ACT(e1, e1, AF.Exp)
        TT("dve", ssum, e0, e1, ALU.add)
        sc.op("dve", lambda e: e.reciprocal(out=ssum.ap, in_=ssum.ap), [ssum], [ssum])
        TT("dve", p0, e0, ssum, ALU.mult)
        TT("dve", p1, e1, ssum, ALU.mult)
        TT("dve", lbs[0], p0, p0, ALU.subtract)
        TT("dve", lbs[1], p0, p1, ALU.add)
        TT("dve", lbs[1], lbs[1], p0, ALU.subtract)
        for l in range(L):
            TS_("dve", lbs[l], lbs[l], 0.0, 1.0, ALU.max, ALU.min)
            TS_("dve", oml[l], lbs[l], -1.0, 1.0, ALU.mult, ALU.add)
        COPY("dve", cact_b, cact)
        COPY("dve", identF.all(), ident)

        s1 = [small[:, 96:104], small[:, 104:112]]
        shf = [small[:, 112:120], small[:, 120:128]]

        def ada(l, bk=None):
            pb = 8 + l * PRM_L
            g_pre, b_shift, b_scale = prm[:, pb:pb + 8], prm[:, pb + 8:pb + 16], prm[:, pb + 16:pb + 24]
            b_gate, g_post = prm[:, pb + 37:pb + 45], prm[:, pb + 45:pb + 53]
            if bk is None:
                bk = bank()
            for u in range(6):
                slot = next_unit(l * UNITS_PER_LAYER + U_ADA + u)
                for c4 in range(4):
                    cc = u * 4 + c4
                    for kc in range(8):
                        MM(bk[:, cc:cc + 1], W(slot, kc, c4 * 128, (c4 + 1) * 128), smallb[:, kc:kc + 1], kc == 0, kc == 7)
                release()
            TT("dve", shf[l], bk[:, 0:8], b_shift, ALU.add)
            tmp = small[:, 128:136]
            TT("dve", tmp, bk[:, 8:16], b_scale, ALU.add)
            STT(s1[l], tmp, 1.0, g_pre, ALU.add, ALU.mult)
            ggf = small[:, 216 + 8 * l: 224 + 8 * l]
            TT("dve", ggf, bk[:, 16:24], b_gate, ALU.add)
            TT("dve", ggf, ggf, g_post, ALU.mult)

        def ada_bcast(l):
            for j in range(8):
                R = arF[:, (j % 2) * 128: (j % 2 + 1) * 128]
                TS_("dve", R, identF.all(), small[:, 216 + 8 * l + j: 217 + 8 * l + j], None, ALU.mult)
                if j % 4 == 0:
                    bk3 = bank()
                MM(bk3[:, (j % 4) * 128: (j % 4 + 1) * 128], onesf.all(), R, True, True)
                if j % 4 == 3:
                    COPY("act", ggt[:, (j // 4) * 512: (j // 4 + 1) * 512], bk3[:, 0:512])

        def layer_seg(l, seg, last):
            pb = 8 + l * PRM_L
            pool_scale = prm[:, pb + 24: pb + 28]
            hng = prm[:, pb + 28: pb + 29]
            t0 = seg * NT
            ssq = small[:, 136:136 + NT]
            rstd = small[:, 152:152 + NT]
            junk = arB[:, OFF_JUNK: OFF_JUNK + 1024]

            for T in range(NT):
                ACT(junk, x[:, t0 + T, :], AF.Square, accum=ssq[:, T:T + 1] if False else small[:, 136 + T:137 + T])
            ACT(rstd, ssq, AF.Sqrt, scale=1.0 / D, bias=EPS)
            sc.op("dve", lambda e: e.reciprocal(out=rstd.ap, in_=rstd.ap), [rstd], [rstd])
            wpv = next_unit(l * UNITS_PER_LAYER + U_PV)
            wpg = next_unit(l * UNITS_PER_LAYER + U_PG)
            pTs = {}

            def s1(T):
                xo = OFF_XN + (T % 2) * 1024
                TS_("dve", arB[:, xo: xo + 1024], x[:, t0 + T, :], small[:, 152 + T:153 + T], None, ALU.mult)
                pT = bbank()
                pTs[T] = pT
                for j in range(8):
                    TR(pT[:, j * 128:(j + 1) * 128], arB[:, xo + j * 128: xo + (j + 1) * 128])

            def s2(T):
                pT = pTs.pop(T)
                for j in range(8):
                    dst = hT[:, j, T * 128:(T + 1) * 128]
                    sc_, bi_ = small[:, 96 + 8 * l + j: 97 + 8 * l + j], small[:, 112 + 8 * l + j: 113 + 8 * l + j]
                    if j % 2 == 0:
                        ACT(dst, pT[:, j * 128:(j + 1) * 128], AF.Identity, scale=sc_, bias=bi_)
                    else:
                        TS_("dve", dst, pT[:, j * 128:(j + 1) * 128], sc_, bi_, ALU.mult, ALU.add)

            def s3(T):
                tok = slice(T * 128, (T + 1) * 128)
                b_pv, b_pg = banks[(T % 2) * 2], banks[(T % 2) * 2 + 1]
                for kc in range(8):
                    MM(b_pv[:, 0:512], hT[:, kc, tok], W(wpv, kc, 0, 512), kc == 0, kc == 7)
                for g in range(4):
                    for kc in range(8):
                        MM(b_pg[:, g * 128:(g + 1) * 128], W(wpg, kc, g * 128, (g + 1) * 128), hT[:, kc, tok], kc == 0, kc == 7)

            def s4(T):
                gt = t0 + T
                tok = slice(T * 128, (T + 1) * 128)
                b_pv, b_pg = banks[(T % 2) * 2], banks[(T % 2) * 2 + 1]
                b_pl, b_p2 = banks[4], banks[5]
                oc, op_ = (gt % 2) * 512, ((gt + 1) % 2) * 512
                COPY("act", arB[:, oc: oc + 512], b_pv[:, 0:512])
                sgo = (T % 2) * 512
                ACT(arF[:, sgo: sgo + 512], b_pg[:, 0:512], AF.Silu)
                for g in range(4):
                    cur = arB[:, oc + g * 128: oc + (g + 1) * 128]
                    prv = arB[:, op_ + g * 128: op_ + (g + 1) * 128]
                    if gt == 0:
                        MM(b_pl[:, g * 128:(g + 1) * 128], cur, PM(8 + g), True, True)
                    else:
                        MM(b_pl[:, g * 128:(g + 1) * 128], cur, PM(g), True, False)
                        MM(b_pl[:, g * 128:(g + 1) * 128], prv, PM(4 + g), False, True)
                plo = OFF_PLS + (T % 2) * 512
                COPY("dve", arB[:, plo: plo + 512], b_pl[:, 0:512])
                for g in range(4):
                    MM(b_p2[:, g * 128:(g + 1) * 128], pw[:, l * 512 + g * 128: l * 512 + (g + 1) * 128],
                       arB[:, plo + g * 128: plo + (g + 1) * 128], True, True)
                for g in range(4):
                    STT(Ain[:, g, tok], b_p2[:, g * 128:(g + 1) * 128], prm[:, pb + 24 + g: pb + 25 + g],
                        arF[:, sgo + g * 128: sgo + (g + 1) * 128], ALU.mult, ALU.mult)

            for t in range(NT + 3):
                l1, l2, l3, l4 = [], [], [], []
                for (lst, fn, T) in ((l1, s1, t), (l2, s2, t - 1), (l3, s3, t - 2), (l4, s4, t - 3)):
                    if 0 <= T < NT:
                        sc.cap = lst
                        fn(T)
                        sc.cap = None
                sc.replay(l2, l1)
                sc.replay(l3, l4)
            release(2)
            def fC(i, par):
                return OFF_FC + (par * 8 + i) * TBW

            def bC(i, par):
                return OFF_C + (par * 7 + i) * TBW

            N64 = TBW // 64
            for par in range(2):
                for (r0, r1, reg) in ((64, 128, 3), (0, 64, 6)):
                    sc.op("dve", lambda e, par=par, r0=r0, r1=r1, reg=reg: e.memset(arB[r0:r1, bC(reg, par): bC(reg, par) + TBW].ap, 0.0), [],
                          [arB[r0:r1, bC(reg, par): bC(reg, par) + TBW]])
            if seg == 0:
                sc.op("dve", lambda e: e.memset(Sst.all().ap, 0.0), [], [Sst.all()])

            def proj(h, tb, wh, pi):
                tok = slice(tb * TBW, (tb + 1) * TBW)
                b1, b2 = banks[(pi % 2) * 2], banks[(pi % 2) * 2 + 1]
                for kc in range(8):
                    MM(b1[:, 0:TBW], W(wh, kc, 0, 128), hT[:, kc, tok], kc == 0, kc == 7)
                for kc in range(8):
                    MM(b1[:, 256:256 + TBW], W(wh, kc, 128, 256), hT[:, kc, tok], kc == 0, kc == 7)
                for kc in range(8):
                    MM(b2[:, 0:TBW], W(wh, kc, 256, 384), hT[:, kc, tok], kc == 0, kc == 7)
                for c in range(NCH):
                    tk = slice(tb * TBW + c * 128, tb * TBW + (c + 1) * 128)
                    for kc in range(8):
                        MM(b2[:, 256 + c * 128: 256 + (c + 1) * 128], hT[:, kc, tk], W(wh, kc, 384, 512), kc == 0, kc == 7)
                return b1, b2

            def scal_views(par):
                so = 232 + par * 24
                er = [small[:, so + 4 + c: so + 5 + c] for c in range(N64)]
                eL = [small[:, so + 8 + c: so + 9 + c] for c in range(N64)]
                aa = [small[:, so + 12 + c: so + 13 + c] for c in range(N64)]
                return so, er, eL, aa

            def prep(h, tb, par, b1, b2):
                omlh = small[:, 64 + 8 * l + h: 65 + 8 * l + h]
                F = lambda i: arF[:, fC(i, par): fC(i, par) + TBW]
                B = lambda i: arB[:, bC(i, par): bC(i, par) + TBW]
                q_sb, kk, sg, logf, cum, E2, lnv = (F(i) for i in range(7))
                v_sb, Qt, Kt = B(0), B(1), B(2)
                zq, zf, zg = b1[:, 0:TBW], b1[:, 256:256 + TBW], b2[:, 0:TBW]
                ACT(kk, zf, AF.Exp)
                ACT(q_sb, zq, AF.Exp, scale=-1.0)
                ACT(sg, zg, AF.Exp, scale=-1.0)
                ACT(kk, kk, AF.Ln, bias=1.0)
                ACT(q_sb, q_sb, AF.Ln, bias=1.0)
                ACT(sg, sg, AF.Ln, bias=1.0)
                ACT(kk, kk, AF.Exp, scale=-1.0)
                ACT(q_sb, q_sb, AF.Exp, scale=-1.0)
                ACT(sg, sg, AF.Exp, scale=-1.0)
                TS_("dve", kk, kk, omlh, None, ALU.mult)
                ACT(logf, kk, AF.Ln, scale=-1.0, bias=1.0)
                TT("dve", q_sb, zq, q_sb, ALU.mult)
                TT("dve", sg, zg, sg, ALU.mult)
                COPY("dve", v_sb, b2[:, 256:256 + TBW])
                so, er, eL, aa = scal_views(par)
                negr = small[:, so: so + N64]
                erv = small[:, so + 4: so + 4 + N64]
                eLv = small[:, so + 8: so + 8 + N64]
                aav = small[:, so + 12: so + 12 + N64]
                lf3 = arF.h[:, fC(3, par): fC(3, par) + TBW].rearrange("p (c t) -> p c t", t=64)[:, :, 0:32]
                sc.op("dve", lambda e: e.tensor_reduce(out=negr.ap, in_=lf3, axis=mybir.AxisListType.X, op=ALU.add, negate=True),
                      [logf], [negr])
                for c in range(N64):
                    cs = slice(fC(4, par) + c * 64, fC(4, par) + (c + 1) * 64)
                    ls = slice(fC(3, par) + c * 64, fC(3, par) + (c + 1) * 64)
                    ini = small[:, so + c: so + c + 1]
                    sc.op("dve", lambda e, cs=cs, ls=ls, ini=ini: e.tensor_tensor_scan(
                        out=arF[:, cs].ap, data0=onesf[:, 0:64].ap, data1=arF[:, ls].ap, initial=ini.ap, op0=ALU.mult, op1=ALU.add),
                        [arF[:, ls], onesf[:, 0:64], ini], [arF[:, cs]])
                ACT(erv, negr, AF.Exp, scale=-1.0)
                dlast = View(arF, arF.h[:, fC(4, par) + 63: fC(4, par) + TBW: 64], fC(4, par), fC(4, par) + TBW)
                ACT(aav, dlast, AF.Exp)
                TT("dve", eLv, aav, erv, ALU.mult)
                ACT(logf, cum, AF.Exp)
                ACT(E2, cum, AF.Exp, scale=-1.0)
                TT("pool", Qt, q_sb, logf, ALU.mult)
                TT("pool", Kt, kk, E2, ALU.mult)

            def mid(h, tb, par, gi, split=None):
                hng = prm[:, pb + 28: pb + 29]
                so, er, eL, aa = scal_views(par)
                F = lambda i: arF[:, fC(i, par): fC(i, par) + TBW]
                B = lambda i: arB[:, bC(i, par): bC(i, par) + TBW]
                sg, lnv = F(2), F(6)
                osq = B(5)

                def Bc(i, c):
                    return arB[:, bC(i, par) + c * 128: bC(i, par) + (c + 1) * 128]

                mA, mB = banks[4], banks[5]
                pk = bbank()
                for c in range(NCH):
                    TR(pk[:, c * 128:(c + 1) * 128], Bc(2, c))
                for c in range(N64):
                    rb = (c % 2) * 64
                    o2, o4 = bC(2, par) + c * 64, bC(1, par) + c * 64
                    MM(mA[rb:rb + 32, c * 64:c * 64 + 32], arB[:, o2:o2 + 32], arB[:, o4:o4 + 32], True, True)
                    MM(mA[rb:rb + 64, c * 64 + 32:c * 64 + 64], arB[:, o2:o2 + 64], arB[:, o4 + 32:o4 + 64], True, True)
                COPY("act", arB[0:64, bC(3, par): bC(3, par) + TBW], pk[0:64, 0:TBW])
                COPY("dve", arB[64:128, bC(6, par): bC(6, par) + TBW], pk[64:128, 0:TBW])
                for c in range(N64):
                    t128 = c // 2
                    kreg = 3 if c % 2 == 0 else 6
                    MM(mB[:, c * 128:(c + 1) * 128], arB[:, bC(kreg, par) + t128 * 128: bC(kreg, par) + (t128 + 1) * 128],
                       arB[:, bC(0, par) + t128 * 128: bC(0, par) + (t128 + 1) * 128], True, True)
                for t128 in range(NCH):
                    TT("dve", Bc(4, t128), mA[:, t128 * 128:(t128 + 1) * 128], maskT, ALU.mult)
                ring = [arF[:, 1152 + k * 128: 1152 + (k + 1) * 128] for k in range(3)]
                tmps = [arF[:, 512 + k * 128: 512 + (k + 1) * 128] for k in range(N64)]
                srcs = [Sst[:, h, :]] + ring
                dsts = ring + [Sst[:, h, :]]
                oo = 256
                Pbs = [arB[:, OFF_PB + c * 128: OFF_PB + (c + 1) * 128] for c in range(N64)]
                for c in range(N64):
                    ACT(tmps[c], mB[:, c * 128:(c + 1) * 128], AF.Identity, scale=aa[c])
                for c in range(N64):
                    TS_("dve", Pbs[c], srcs[c], er[c], None, ALU.mult)
                    STT(dsts[c], srcs[c], eL[c], tmps[c], ALU.mult, ALU.add)
                if split is not None:
                    split.append(len(sc.cap))
                for c in range(N64):
                    t128 = c // 2
                    if c % 2 == 0:
                        MM(mA[:, oo + t128 * 128: oo + (t128 + 1) * 128], Bc(0, t128), Bc(4, t128), True, False)
                    MM(mA[:, oo + c * 64: oo + (c + 1) * 64], Pbs[c], arB[:, bC(1, par) + c * 64: bC(1, par) + (c + 1) * 64], False, c % 2 == 1)
                ACT(osq, mA[:, oo:oo + TBW], AF.Square)
                STT(F(7), mA[:, oo:oo + TBW], hng, sg, ALU.mult, ALU.mult)

            def norm(h, tb, par):
                F = lambda i: arF[:, fC(i, par): fC(i, par) + TBW]
                lnv, ogp = F(6), F(7)
                osq = arB[:, bC(5, par): bC(5, par) + TBW]
                mB = banks[5]
                la_ = []
                sc.cap = la_
                MM(mB[:, 0:TBW], ones_b, osq, True, True)
                ACT(lnv, mB[:, 0:TBW], AF.Ln, scale=1.0 / 128, bias=EPS)
                ACT(lnv, lnv, AF.Exp, scale=-0.5)
                sc.cap = None
                lb_ = []
                sc.cap = lb_
                TT("dve", Og[:, h, tb * TBW:(tb + 1) * TBW], ogp, lnv, ALU.mult)
                sc.cap = None
                return la_, lb_

            items = [(h, tb) for h in range(NH) for tb in range(NTB)]
            nI = len(items)
            pbanks = {}
            whs = {}
            for i in range(nI + 3):
                la, lb, lc = [], [], []
                rel = False
                if i < nI:
                    h, tb = items[i]
                    if tb == 0:
                        if h == NH // 2 and seg == 0 and l + 1 < L:
                            ada(l + 1, banks[5])
                        whs[h] = next_unit(l * UNITS_PER_LAYER + U_HEAD + h)
                    sc.cap = lc
                    pbanks[i] = proj(h, tb, whs[h], i)
                    sc.cap = None
                    rel = tb == NTB - 1
                if 0 <= i - 1 < nI:
                    h1, tb1 = items[i - 1]
                    sc.cap = la
                    prep(h1, tb1, (i - 1) % 2, *pbanks.pop(i - 1))
                    sc.cap = None
                split = []
                if 0 <= i - 2 < nI:
                    h2, tb2 = items[i - 2]
                    sc.cap = lb
                    mid(h2, tb2, (i - 2) % 2, (seg * nI + i - 2), split)
                    sc.cap = None
                n1, n2 = [], []
                if 0 <= i - 3 < nI:
                    h3, tb3 = items[i - 3]
                    n1, n2 = norm(h3, tb3, (i - 3) % 2)
                k = split[0] if split else 0
                sc.replay(n1)
                sc.replay(lb[:k])
                sc.replay(n2)
                sc.replay(lc, la)
                sc.replay(lb[k:])
                if rel:
                    release()

            for j in range(8):
                wd = next_unit(l * UNITS_PER_LAYER + U_D + j)
                for tb in range(TS // 512 if TS >= 512 else 1):
                    wdt = min(512, TS)
                    tok = slice(tb * wdt, (tb + 1) * wdt)
                    g1, g2, ba, bb = bank(), bank(), bank(), bank()
                    for kc in range(8):
                        MM(g1[:, 0:wdt], wd[:, kc * 128:(kc + 1) * 128], hT[:, kc, tok], kc == 0, kc == 7)
                    for kc in range(8):
                        MM(g2[:, 0:wdt], wd[:, 1024 + kc * 128: 1024 + (kc + 1) * 128], hT[:, kc, tok], kc == 0, kc == 7)
                    for kc in range(4):
                        MM(ba[:, 0:wdt], wd[:, 3072 + kc * 128: 3072 + (kc + 1) * 128], Ain[:, kc, tok], kc == 0, kc == 3)
                    for kc in range(8):
                        MM(bb[:, 0:wdt], wd[:, 2048 + kc * 128: 2048 + (kc + 1) * 128], Og[:, kc, tok], kc == 0, kc == 7)
                    par = tb % 2
                    sp_ = arF[:, par * 1024: par * 1024 + wdt]
                    sh_ = arF[:, par * 1024 + 512: par * 1024 + 512 + wdt]
                    ACT(sp_, g1[:, 0:wdt], AF.Sigmoid)
                    ACT(sh_, g2[:, 0:wdt], AF.Sigmoid)
                    TT("dve", sp_, ba[:, 0:wdt], sp_, ALU.mult)
                    TT("dve", sh_, bb[:, 0:wdt], sh_, ALU.mult)
                    TT("dve", mer[:, j, tok], sp_, sh_, ALU.add)
                release()

            wo0 = next_unit(l * UNITS_PER_LAYER + U_OUT)
            wo1 = next_unit(l * UNITS_PER_LAYER + U_OUT + 1)
            for T in range(NT):
                gt = t0 + T
                tok = slice(T * 128, (T + 1) * 128)
                y0, y1 = bank(), bank()
                for kc in range(8):
                    MM(y0[:, 0:512], mer[:, kc, tok], W(wo0, kc, 0, 512), kc == 0, kc == 7)
                for kc in range(8):
                    MM(y1[:, 0:512], mer[:, kc, tok], W(wo1, kc, 0, 512), kc == 0, kc == 7)
                so = 200 + (T % 2) * 4
                sa, sb_, rs = small[:, so:so + 1], small[:, so + 1:so + 2], small[:, so + 2:so + 3]
                ACT(arB[:, OFF_JUNK: OFF_JUNK + 512], y0[:, 0:512], AF.Square, accum=sa)
                ACT(arB[:, OFF_JUNK + 512: OFF_JUNK + 1024], y1[:, 0:512], AF.Square, accum=sb_)
                TT("dve", rs, sa, sb_, ALU.add)
                ACT(rs, rs, AF.Sqrt, scale=1.0 / D, bias=EPS)
                sc.op("dve", lambda e, rs=rs: e.reciprocal(out=rs.ap, in_=rs.ap), [rs], [rs])
                tmp = arF[:, (T % 2) * 1024: (T % 2 + 1) * 1024]
                STT(arF[:, (T % 2) * 1024: (T % 2) * 1024 + 512], y0[:, 0:512], rs, ggt[:, 0:512], ALU.mult, ALU.mult)
                STT(arF[:, (T % 2) * 1024 + 512: (T % 2 + 1) * 1024], y1[:, 0:512], rs, ggt[:, 512:1024], ALU.mult, ALU.mult)
                TT("pool", x[:, gt, :], x[:, gt, :], tmp, ALU.add)
                if last:
                    sc.dma("sp", f"o{gt % 4}", OUTd[gt * 128:(gt + 1) * 128, :], x[:, gt, :])
            release(2)

        ada(0)
        for l in range(nlayers):
            ada_bcast(l)
            for seg in range(NSEG):
                layer_seg(l, seg, last=(l == nlayers - 1))
        sc.wait_all("sp", [f"o{i}" for i in range(4)])
        sc.emit(st)
    return nc


def _pool_mats():
    pm = np.zeros((12, 128, 128), np.float32)
    s = np.arange(128)[:, None]
    t = np.arange(128)[None, :]
    for g, w in enumerate((2, 4, 8, 16)):
        eye = (s == t).astype(np.float32)
        pm[g] = ((s <= t) & (s > t - w)) / w - eye
        pm[4 + g] = ((s - 128) > (t - w)) / w
        cnt = np.minimum(t + 1, w)
        pm[8 + g] = ((s <= t) & (s > t - w)) / cnt - eye
    return pm


def _consts():
    cbm = np.zeros((128, 1920), np.float32)
    cbm[:, 0:128] = np.eye(128)
    s = np.arange(128)[:, None]
    t = np.arange(128)[None, :]
    cbm[:, 128:256] = (s <= t) & ((s // 64) == (t // 64))
    cbm[:, 256:384] = 1.0
    pm = _pool_mats()
    for i in range(12):
        cbm[:, 384 + i * 128: 384 + (i + 1) * 128] = pm[i]
    return cbm.astype(ml_dtypes.bfloat16)


def _fm(v):
    return np.ascontiguousarray(v.reshape(8, 128).T)


def _kunit(wcols):
    n = wcols.shape[1]
    return wcols.reshape(-1, 128, n).transpose(1, 0, 2).reshape(128, -1)


def _pack_weights(w_ada, w_in, w_pool_o, w_hgrn_o, w_out):
    wu = np.zeros((L * UNITS_PER_LAYER, 128, 4096), np.float32)
    for l in range(L):
        b = l * UNITS_PER_LAYER
        for i in range(6):
            wu[b + U_ADA + i] = _kunit(w_ada[l][:, i * 512:(i + 1) * 512])
        wi = w_in[l]
        wu[b + U_PV] = _kunit(wi[:, 0:512])
        wu[b + U_PG] = _kunit(wi[:, 512:1024])
        for h in range(NH):
            cols = np.concatenate([wi[:, 1024 + h * 128: 1024 + (h + 1) * 128],
                                   wi[:, 2048 + h * 128: 2048 + (h + 1) * 128],
                                   wi[:, 4096 + h * 128: 4096 + (h + 1) * 128],
                                   wi[:, 3072 + h * 128: 3072 + (h + 1) * 128]], 1)
            wu[b + U_HEAD + h] = _kunit(cols)
        for j in range(8):
            js = slice(j * 128, (j + 1) * 128)
            wu[b + U_D + j, :, 0:1024] = _kunit(wi[:, 5120:6144][:, js])
            wu[b + U_D + j, :, 1024:2048] = _kunit(wi[:, 6144:7168][:, js])
            wu[b + U_D + j, :, 2048:3072] = _kunit(w_hgrn_o[l][:, js])
            wu[b + U_D + j, :, 3072:3584] = _kunit(w_pool_o[l][:, js])
        wu[b + U_OUT] = _kunit(w_out[l][:, 0:512])
        wu[b + U_OUT + 1] = _kunit(w_out[l][:, 512:1024])
    return wu


def _pack_params(c_row, b_ada, g_pre, g_post, pool_scale, lb_logits, hgrn_norm_g, pool_w):
    prm = np.zeros((128, NPRM), np.float32)
    prm[:, 0:8] = _fm(c_row)
    pwm = np.zeros((128, L * 512), np.float32)
    for l in range(L):
        pb = 8 + l * PRM_L
        prm[:, pb:pb + 8] = _fm(g_pre[l])
        prm[:, pb + 8:pb + 16] = _fm(b_ada[l][0:1024])
        prm[:, pb + 16:pb + 24] = _fm(b_ada[l][1024:2048])
        prm[:, pb + 24:pb + 28] = pool_scale[l].reshape(4, 128).T
        prm[:, pb + 28] = hgrn_norm_g[l] * np.float32(1.0)
        prm[:, pb + 29:pb + 37] = _fm(lb_logits[l])
        prm[:, pb + 37:pb + 45] = _fm(b_ada[l][2048:3072])
        prm[:, pb + 45:pb + 53] = _fm(g_post[l])
        pwm[:, l * 512:(l + 1) * 512] = pool_w[l].transpose(1, 0, 2).reshape(128, 512)
    return prm, pwm


_CACHE = {}


def make_in_maps(x, c, w_ada, b_ada, g_pre, g_post, w_in, pool_w, pool_scale, lb_logits,
                 hgrn_norm_g, w_pool_o, w_hgrn_o, w_out):
    f = lambda a: np.asarray(a, dtype=np.float32)
    x, c, w_ada, b_ada, g_pre, g_post, w_in, pool_w, pool_scale, lb_logits, hgrn_norm_g, w_pool_o, w_hgrn_o, w_out = map(
        f, (x, c, w_ada, b_ada, g_pre, g_post, w_in, pool_w, pool_scale, lb_logits, hgrn_norm_g, w_pool_o, w_hgrn_o, w_out))
    wu = _pack_weights(w_ada, w_in, w_pool_o, w_hgrn_o, w_out)
    cbm = _consts()
    maps = []
    for b in range(x.shape[0]):
        prm, pwm = _pack_params(c[b], b_ada, g_pre, g_post, pool_scale, lb_logits, hgrn_norm_g, pool_w)
        maps.append({"x": np.ascontiguousarray(x[b]), "wu": wu, "prm": prm, "cb": cbm, "pw": pwm})
    return maps


def kernel(**inputs):
    x = np.asarray(inputs["x"])
    B, S, _ = x.shape
    maps = make_in_maps(**inputs)
    if S not in _CACHE:
        _CACHE[S] = build(S)
    res = run_bass_kernel_spmd(_CACHE[S], maps, core_ids=list(range(B)))
    return np.stack([np.asarray(r["out"], dtype=np.float32) for r in res.results], 0)
```

```python
import numpy as np
import ml_dtypes
from contextlib import ExitStack
import concourse.bass as bass
import concourse.mybir as mybir
from concourse.bass_utils import run_bass_kernel_spmd

F32 = mybir.dt.float32
BF16 = mybir.dt.bfloat16
AF = mybir.ActivationFunctionType
ALU = mybir.AluOpType

D = 1024
L = 2
NH = 8
EPS = 1e-6
UNITS_PER_LAYER = 26
U_ADA, U_PV, U_PG, U_HEAD, U_D, U_OUT = 0, 6, 7, 8, 16, 24
PRM_L = 56
NPRM = 8 + L * PRM_L
NSLOT = 4


class View:
    __slots__ = ("buf", "ap", "lo", "hi")

    def __init__(self, buf, ap, lo, hi):
        self.buf, self.ap, self.lo, self.hi = buf, ap, lo, hi


class Buf:
    def __init__(self, name, handle, shape, space):
        self.name, self.h, self.shape, self.space = name, handle, list(shape), space
        fs = self.shape if space == "dram" else self.shape[1:]
        st, acc = [], 1
        for n in reversed(fs):
            st.append(acc)
            acc *= n
        self.fstr = list(reversed(st))
        self.fsize = acc
        self.acc = []
        self.excl = space == "psum"

    def __getitem__(self, idx):
        if not isinstance(idx, tuple):
            idx = (idx,)
        idx = tuple(idx) + (slice(None),) * (len(self.shape) - len(idx))
        fidx = idx if self.space == "dram" else idx[1:]
        fsh = self.shape if self.space == "dram" else self.shape[1:]
        lo = hi = 0
        for k, (i, n) in enumerate(zip(fidx, fsh)):
            if isinstance(i, slice):
                a = 0 if i.start is None else i.start
                b = n if i.stop is None else i.stop
            else:
                a, b = i, i + 1
            lo += a * self.fstr[k]
            hi += (b - 1) * self.fstr[k]
        return View(self, self.h[idx], lo, hi + 1)

    def all(self):
        return self[tuple(slice(None) for _ in self.shape)]


class _Op:
    __slots__ = ("eng", "fn", "waits", "res", "seq", "is_dma")


class Sched:
    COMPUTE = ("pe", "act", "dve", "pool")

    def __init__(self, nc):
        self.nc = nc
        self.ops = {e: [] for e in ("pe", "act", "dve", "pool", "sp")}
        self.count = {}
        self.known = {e: {} for e in self.ops}
        self.snaps = {}
        self.signalled = {}
        self.chan_names = []
        self.cap = None

    def replay(self, *lists):
        keyed = []
        for li, lst in enumerate(lists):
            keyed += [((k + 0.5) / len(lst), li, k, o) for k, o in enumerate(lst)]
        keyed.sort(key=lambda t: (t[0], t[1], t[2]))
        for _, _, _, (eng, fn, reads, writes) in keyed:
            self.op(eng, fn, reads, writes)

    def _deps(self, eng, reads, writes):
        deps = {}

        def add(r, s, raw):
            if r == eng and eng == "pe":
                return
            if deps.get(r, -1) < s:
                deps[r] = s

        for v in reads:
            for (r, s, lo, hi, w) in v.buf.acc:
                if v.buf.excl:
                    add(r, s, False)
                elif w and lo < v.hi and v.lo < hi:
                    add(r, s, True)
        for v in writes:
            for (r, s, lo, hi, w) in v.buf.acc:
                if v.buf.excl or (lo < v.hi and v.lo < hi):
                    add(r, s, False)
        return deps

    def _log(self, res, seq, reads, writes):
        for v in reads:
            if v.buf.excl:
                v.buf.acc = [(res, seq, 0, v.buf.fsize, True)]
            else:
                v.buf.acc.append((res, seq, v.lo, v.hi, False))
        for v in writes:
            b = v.buf
            if b.excl:
                b.acc = [(res, seq, 0, b.fsize, True)]
                continue
            b.acc = [a for a in b.acc if not (a[2] >= v.lo and a[3] <= v.hi)]
            b.acc.append((res, seq, v.lo, v.hi, True))

    def _resolve(self, eng, deps):
        kn = self.known[eng]
        waits = []
        for r, s in sorted(deps.items()):
            if kn.get(r, -1) >= s:
                continue
            waits.append((r, s))
            self.signalled.setdefault(r, set()).add(s)
        for r, s in waits:
            for k2, v2 in self.snaps[(r, s)].items():
                if kn.get(k2, -1) < v2:
                    kn[k2] = v2
            if kn.get(r, -1) < s:
                kn[r] = s
        return waits

    def op(self, eng, fn, reads=(), writes=()):
        if self.cap is not None:
            self.cap.append((eng, fn, list(reads), list(writes)))
            return None
        deps = self._deps(eng, reads, writes)
        o = _Op()
        o.eng, o.fn, o.res, o.is_dma = eng, fn, eng, False
        o.waits = self._resolve(eng, deps)
        o.seq = self.count.get(eng, 0)
        self.count[eng] = o.seq + 1
        self.snaps[(eng, o.seq)] = dict(self.known[eng])
        self._log(eng, o.seq, reads, writes)
        self.ops[eng].append(o)
        return o

    def dma(self, q, chan, out, in_, **kw):
        if chan not in self.chan_names:
            self.chan_names.append(chan)
        reads, writes = [in_], [out]
        deps = self._deps(q, reads, writes)
        seq = self.count.get(chan, 0)
        if seq > 0:
            deps[chan] = max(deps.get(chan, -1), seq - 1)
        o = _Op()
        o.eng, o.res, o.is_dma, o.seq = q, chan, True, seq
        oa, ia = out.ap, in_.ap
        o.fn = lambda e, oa=oa, ia=ia, kw=kw: e.dma_start(out=oa, in_=ia, **kw)
        o.waits = self._resolve(q, deps)
        self.count[chan] = seq + 1
        self.snaps[(chan, seq)] = dict(self.known[q])
        self._log(chan, seq, reads, writes)
        self.ops[q].append(o)
        return o

    def wait_all(self, eng, chans):
        deps = {c: self.count[c] - 1 for c in chans if self.count.get(c, 0) > 0}
        o = _Op()
        o.eng, o.fn, o.res, o.is_dma, o.seq = eng, None, None, False, -1
        o.waits = self._resolve(eng, deps)
        self.ops[eng].append(o)

    def emit(self, stack):
        nc = self.nc
        sems = {}
        for r in list(self.COMPUTE) + self.chan_names:
            sems[r] = stack.enter_context(nc.semaphore("s_" + r))
        cum = {}
        for r in self.COMPUTE:
            sig = sorted(self.signalled.get(r, ()))
            cum[r] = {s: i + 1 for i, s in enumerate(sig)}

        def val(r, s):
            return cum[r][s] if r in self.COMPUTE else 16 * (s + 1)

        block = stack.enter_context(nc.Block())
        handles = {"pe": block.tensor, "act": block.scalar, "dve": block.vector,
                   "pool": block.gpsimd, "sp": block.sync}
        for eng, dec in handles.items():
            ops = self.ops[eng]

            def body(e, ops=ops, eng=eng):
                for o in ops:
                    for (r, s) in o.waits:
                        e.wait_ge(sems[r], val(r, s))
                    if o.fn is None:
                        continue
                    ins = o.fn(e)
                    if o.is_dma:
                        ins.then_inc(sems[o.res], 16)
                    elif o.seq in cum[eng]:
                        ins.then_inc(sems[eng], 1)

            dec(body)


def unit_schedule(nseg):
    out = []
    for l in range(L):
        b = l * UNITS_PER_LAYER
        if l == 0:
            out += [b + U_ADA + i for i in range(6)]
        for seg in range(nseg):
            out += [b + U_PV, b + U_PG]
            for h in range(NH):
                if h == NH // 2 and seg == 0 and l + 1 < L:
                    out += [(l + 1) * UNITS_PER_LAYER + U_ADA + i for i in range(6)]
                out.append(b + U_HEAD + h)
            out += [b + U_D + j for j in range(8)]
            out += [b + U_OUT, b + U_OUT + 1]
    return out


def build(S, nlayers=L):
    assert S % 256 == 0
    NTILE = S // 128
    TS = min(1024, S)
    NSEG = S // TS
    NT = TS // 128
    TBW = 256
    NCH = TBW // 128
    NTB = TS // TBW

    nc = bass.Bass("TRN2", target_bir_lowering=False)
    xd = nc.dram_tensor("x", [S, D], F32, kind="ExternalInput")
    wud = nc.dram_tensor("wu", [L * UNITS_PER_LAYER, 128, 4096], F32, kind="ExternalInput")
    prmd = nc.dram_tensor("prm", [128, NPRM], F32, kind="ExternalInput")
    cbd = nc.dram_tensor("cb", [128, 384 + 1536], BF16, kind="ExternalInput")
    pwd = nc.dram_tensor("pw", [128, L * 512], F32, kind="ExternalInput")
    outd = nc.dram_tensor("out", [S, D], F32, kind="ExternalOutput")

    with ExitStack() as st:
        def sb(name, shape, dt):
            return Buf(name, st.enter_context(nc.sbuf_tensor(name, shape, dt)), shape, "sbuf")

        def ps(name, shape, dt):
            return Buf(name, st.enter_context(nc.psum_tensor(name, shape, dt)), shape, "psum")

        Xd = Buf("x_d", xd, [S, D], "dram")
        WUd = Buf("wu_d", wud, [L * UNITS_PER_LAYER, 128, 4096], "dram")
        PRMd = Buf("prm_d", prmd, [128, NPRM], "dram")
        CBd = Buf("cb_d", cbd, [128, 1920], "dram")
        PWd = Buf("pw_d", pwd, [128, L * 512], "dram")
        OUTd = Buf("out_d", outd, [S, D], "dram")

        sc = Sched(nc)

        x = sb("xres", [128, NTILE, D], F32)
        hT = sb("hT", [128, 8, TS], BF16)
        Ain = sb("Ain", [128, 4, TS], BF16)
        Og = sb("Og", [128, 8, TS], BF16)
        mer = sb("mer", [128, 8, TS], BF16)
        slots = [sb(f"wslot{i}", [128, 4096], BF16) for i in range(NSLOT)]
        prm = sb("prm_s", [128, NPRM], F32)
        identF = sb("identF", [128, 128], F32)
        cb = sb("cb_s", [128, 1920], BF16)
        pw = sb("pw_s", [128, L * 512], BF16)
        ggt = sb("ggt", [128, D], F32)
        Sst = sb("Sst", [128, NH, 128], F32)
        onesf = sb("onesf", [128, 128], F32)
        small = sb("small", [128, 320], F32)
        OFF_FC = 1536
        NF = max(2048, OFF_FC + 16 * TBW)
        OFF_XN, OFF_PLS, OFF_C = 1024, 3072, 1024
        OFF_PB = OFF_C + 14 * TBW
        OFF_JUNK = OFF_PB + 512
        NB = OFF_JUNK + 1024
        smallb = sb("smallb", [128, 16], BF16)
        arF = sb("arF", [128, NF], F32)
        arB = sb("arB", [128, NB], BF16)

        banks = [ps(f"pb{i}", [128, 512], F32) for i in range(6)]
        bbanks = [ps(f"pbb{i}", [128, 1024], BF16) for i in range(2)]
        rot = {"f": 0, "b": 0}

        def bank():
            b = banks[rot["f"] % 6]
            rot["f"] += 1
            return b

        def bbank():
            b = bbanks[rot["b"] % 2]
            rot["b"] += 1
            return b

        ident = cb[:, 0:128]
        maskT = cb[:, 128:256]
        ones_b = cb[:, 256:384]

        def PM(i):
            return cb[:, 384 + i * 128: 384 + (i + 1) * 128]

        def _v(a):
            return isinstance(a, View)

        def ACT(out, in_, func, scale=None, bias=None, accum=None):
            kw, rd, wr = {}, [in_], [out]
            if scale is not None:
                kw["scale"] = scale.ap if _v(scale) else scale
                if _v(scale):
                    rd.append(scale)
            if bias is not None:
                kw["bias"] = bias.ap if _v(bias) else bias
                if _v(bias):
                    rd.append(bias)
            if accum is not None:
                kw["accum_out"] = accum.ap
                wr.append(accum)
            sc.op("act", lambda e: e.activation(out=out.ap, in_=in_.ap, func=func, **kw), rd, wr)

        def TS_(eng, out, in0, s1, s2, op0, op1=None):
            rd = [in0] + [s for s in (s1, s2) if _v(s)]
            a1 = s1.ap if _v(s1) else s1
            a2 = s2.ap if _v(s2) else s2
            if op1 is None:
                sc.op(eng, lambda e: e.tensor_scalar(out=out.ap, in0=in0.ap, scalar1=a1, scalar2=None, op0=op0), rd, [out])
            else:
                sc.op(eng, lambda e: e.tensor_scalar(out=out.ap, in0=in0.ap, scalar1=a1, scalar2=a2, op0=op0, op1=op1), rd, [out])

        def TT(eng, out, in0, in1, op):
            sc.op(eng, lambda e: e.tensor_tensor(out=out.ap, in0=in0.ap, in1=in1.ap, op=op), [in0, in1], [out])

        def STT(out, in0, scalar, in1, op0, op1):
            rd = [in0, in1] + ([scalar] if _v(scalar) else [])
            s_ = scalar.ap if _v(scalar) else scalar
            sc.op("dve", lambda e: e.scalar_tensor_tensor(out=out.ap, in0=in0.ap, scalar=s_, in1=in1.ap, op0=op0, op1=op1), rd, [out])

        def MM(out, lhsT, rhs, start, stop):
            sc.op("pe", lambda e: e.matmul(out=out.ap, lhsT=lhsT.ap, rhs=rhs.ap, start=start, stop=stop), [lhsT, rhs], [out])

        def TR(out, in_):
            sc.op("pe", lambda e: e.transpose(out=out.ap, in_=in_.ap, identity=ident.ap), [in_, ident], [out])

        def COPY(eng, out, in_):
            if eng == "act":
                sc.op("act", lambda e: e.copy(out=out.ap, in_=in_.ap), [in_], [out])
            else:
                sc.op(eng, lambda e: e.tensor_copy(out=out.ap, in_=in_.ap), [in_], [out])

        usched = unit_schedule(NSEG)
        ust = {"next_use": 0, "next_dma": 0}

        def _issue_unit(i):
            u = usched[i]
            slot = slots[i % NSLOT]
            src = View(WUd, wud[u].rearrange("p (a b) -> p a b", b=2048), u * 128 * 4096, (u + 1) * 128 * 4096)
            dst = View(slot, slot.h[:, :].rearrange("p (a b) -> p a b", b=2048), 0, 4096)
            sc.dma("pool", f"w{i % NSLOT}", dst, src)

        def next_unit(expect):
            i = ust["next_use"]
            assert usched[i] == expect, (i, usched[i], expect)
            while ust["next_dma"] < min(len(usched), NSLOT):
                _issue_unit(ust["next_dma"])
                ust["next_dma"] += 1
            assert i < ust["next_dma"]
            ust["next_use"] = i + 1
            return slots[i % NSLOT]

        def release(n=1):
            for _ in range(n):
                if ust["next_dma"] < len(usched):
                    _issue_unit(ust["next_dma"])
                    ust["next_dma"] += 1

        def W(slot, kc, a, b, width=512):
            return slot[:, kc * width + a: kc * width + b]

        sc.dma("sp", "prm", prm.all(), PRMd.all())
        sc.dma("sp", "prm", cb.all(), CBd.all())
        sc.dma("pool", "pwc", pw.all(), PWd.all())
        for t in range(NTILE):
            sc.dma("sp", f"x{t % 4}", x[:, t, :], Xd[t * 128:(t + 1) * 128, :])

        for bkz in banks:
            sc.op("dve", lambda e, bkz=bkz: e.memset(bkz.all().ap, 0.0), [], [bkz.all()])
        sc.op("dve", lambda e: e.memset(onesf.all().ap, 1.0), [], [onesf.all()])
        cact = small[:, 0:8]
        ACT(cact, prm[:, 0:8], AF.Silu)
        cact_b = smallb[:, 0:8]
        assert L == 2
        l0 = prm[:, 8 + 29: 8 + 37]
        l1 = prm[:, 8 + PRM_L + 29: 8 + PRM_L + 37]
        mx, e0, e1, ssum, p0, p1 = (small[:, 8 + 8 * i: 16 + 8 * i] for i in range(6))
        oml = [small[:, 64:72], small[:, 72:80]]
        lbs = [small[:, 80:88], small[:, 88:96]]
        TT("dve", mx, l0, l1, ALU.max)
        TT("dve", e0, l0, mx, ALU.subtract)
        TT("dve", e1, l1, mx, ALU.subtract)
        ACT(e0, e0, AF.Exp)
        ACT(e1, e1, AF.Exp)
        TT("dve", ssum, e0, e1, ALU.add)
        sc.op("dve", lambda e: e.reciprocal(out=ssum.ap, in_=ssum.ap), [ssum], [ssum])
        TT("dve", p0, e0, ssum, ALU.mult)
        TT("dve", p1, e1, ssum, ALU.mult)
        TT("dve", lbs[0], p0, p0, ALU.subtract)
        TT("dve", lbs[1], p0, p1, ALU.add)
        TT("dve", lbs[1], lbs[1], p0, ALU.subtract)
        for l in range(L):
            TS_("dve", lbs[l], lbs[l], 0.0, 1.0, ALU.max, ALU.min)
            TS_("dve", oml[l], lbs[l], -1.0, 1.0, ALU.mult, ALU.add)
        COPY("dve", cact_b, cact)
        COPY("dve", identF.all(), ident)

        s1 = [small[:, 96:104], small[:, 104:112]]
        shf = [small[:, 112:120], small[:, 120:128]]

        def ada(l, bk=None):
            pb = 8 + l * PRM_L
            g_pre, b_shift, b_scale = prm[:, pb:pb + 8], prm[:, pb + 8:pb + 16], prm[:, pb + 16:pb + 24]
            b_gate, g_post = prm[:, pb + 37:pb + 45], prm[:, pb + 45:pb + 53]
            if bk is None:
                bk = bank()
            for u in range(6):
                slot = next_unit(l * UNITS_PER_LAYER + U_ADA + u)
                for c4 in range(4):
                    cc = u * 4 + c4
                    for kc in range(8):
                        MM(bk[:, cc:cc + 1], W(slot, kc, c4 * 128, (c4 + 1) * 128), smallb[:, kc:kc + 1], kc == 0, kc == 7)
                release()
            TT("dve", shf[l], bk[:, 0:8], b_shift, ALU.add)
            tmp = small[:, 128:136]
            TT("dve", tmp, bk[:, 8:16], b_scale, ALU.add)
            STT(s1[l], tmp, 1.0, g_pre, ALU.add, ALU.mult)
            ggf = small[:, 216 + 8 * l: 224 + 8 * l]
            TT("dve", ggf, bk[:, 16:24], b_gate, ALU.add)
            TT("dve", ggf, ggf, g_post, ALU.mult)

        def ada_bcast(l):
            for j in range(8):
                R = arF[:, (j % 2) * 128: (j % 2 + 1) * 128]
                TS_("dve", R, identF.all(), small[:, 216 + 8 * l + j: 217 + 8 * l + j], None, ALU.mult)
                if j % 4 == 0:
                    bk3 = bank()
                MM(bk3[:, (j % 4) * 128: (j % 4 + 1) * 128], onesf.all(), R, True, True)
                if j % 4 == 3:
                    COPY("act", ggt[:, (j // 4) * 512: (j // 4 + 1) * 512], bk3[:, 0:512])

        def prologue(seg):
            ssq = small[:, 136:136 + NT]
            rstd = small[:, 152:152 + NT]
            junk = arB[:, OFF_JUNK: OFF_JUNK + 1024]
            for T in range(NT):
                ACT(junk, x[:, seg * NT + T, :], AF.Square, accum=small[:, 136 + T:137 + T])
            ACT(rstd, ssq, AF.Sqrt, scale=1.0 / D, bias=EPS)
            sc.op("dve", lambda e: e.reciprocal(out=rstd.ap, in_=rstd.ap), [rstd], [rstd])

        def layer_seg(l, seg, last, nxt_seg=None):
            pb = 8 + l * PRM_L
            pool_scale = prm[:, pb + 24: pb + 28]
            hng = prm[:, pb + 28: pb + 29]
            t0 = seg * NT
            ssq = small[:, 136:136 + NT]
            rstd = small[:, 152:152 + NT]
            junk = arB[:, OFF_JUNK: OFF_JUNK + 1024]

            wpv = next_unit(l * UNITS_PER_LAYER + U_PV)
            wpg = next_unit(l * UNITS_PER_LAYER + U_PG)
            pTs = {}

            def s1(T):
                xo = OFF_XN + (T % 2) * 1024
                TS_("dve", arB[:, xo: xo + 1024], x[:, t0 + T, :], small[:, 152 + T:153 + T], None, ALU.mult)
                pT = bbank()
                pTs[T] = pT
                for j in range(8):
                    TR(pT[:, j * 128:(j + 1) * 128], arB[:, xo + j * 128: xo + (j + 1) * 128])

            def s2(T):
                pT = pTs.pop(T)
                for j in range(8):
                    dst = hT[:, j, T * 128:(T + 1) * 128]
                    sc_, bi_ = small[:, 96 + 8 * l + j: 97 + 8 * l + j], small[:, 112 + 8 * l + j: 113 + 8 * l + j]
                    if j % 2 == 0:
                        ACT(dst, pT[:, j * 128:(j + 1) * 128], AF.Identity, scale=sc_, bias=bi_)
                    else:
                        TS_("dve", dst, pT[:, j * 128:(j + 1) * 128], sc_, bi_, ALU.mult, ALU.add)

            def s3(T):
                tok = slice(T * 128, (T + 1) * 128)
                b_pv, b_pg = banks[(T % 2) * 2], banks[(T % 2) * 2 + 1]
                for kc in range(8):
                    MM(b_pv[:, 0:512], hT[:, kc, tok], W(wpv, kc, 0, 512), kc == 0, kc == 7)
                for g in range(4):
                    for kc in range(8):
                        MM(b_pg[:, g * 128:(g + 1) * 128], W(wpg, kc, g * 128, (g + 1) * 128), hT[:, kc, tok], kc == 0, kc == 7)

            def s4(T):
                gt = t0 + T
                tok = slice(T * 128, (T + 1) * 128)
                b_pv, b_pg = banks[(T % 2) * 2], banks[(T % 2) * 2 + 1]
                b_pl, b_p2 = banks[4], banks[5]
                oc, op_ = (gt % 2) * 512, ((gt + 1) % 2) * 512
                COPY("act", arB[:, oc: oc + 512], b_pv[:, 0:512])
                sgo = (T % 2) * 512
                ACT(arF[:, sgo: sgo + 512], b_pg[:, 0:512], AF.Silu)
                for g in range(4):
                    cur = arB[:, oc + g * 128: oc + (g + 1) * 128]
                    prv = arB[:, op_ + g * 128: op_ + (g + 1) * 128]
                    if gt == 0:
                        MM(b_pl[:, g * 128:(g + 1) * 128], cur, PM(8 + g), True, True)
                    else:
                        MM(b_pl[:, g * 128:(g + 1) * 128], cur, PM(g), True, False)
                        MM(b_pl[:, g * 128:(g + 1) * 128], prv, PM(4 + g), False, True)
                plo = OFF_PLS + (T % 2) * 512
                COPY("dve", arB[:, plo: plo + 512], b_pl[:, 0:512])
                for g in range(4):
                    MM(b_p2[:, g * 128:(g + 1) * 128], pw[:, l * 512 + g * 128: l * 512 + (g + 1) * 128],
                       arB[:, plo + g * 128: plo + (g + 1) * 128], True, True)
                for g in range(4):
                    STT(Ain[:, g, tok], b_p2[:, g * 128:(g + 1) * 128], prm[:, pb + 24 + g: pb + 25 + g],
                        arF[:, sgo + g * 128: sgo + (g + 1) * 128], ALU.mult, ALU.mult)

            for t in range(NT + 3):
                l1, l2, l3, l4 = [], [], [], []
                for (lst, fn, T) in ((l1, s1, t), (l2, s2, t - 1), (l3, s3, t - 2), (l4, s4, t - 3)):
                    if 0 <= T < NT:
                        sc.cap = lst
                        fn(T)
                        sc.cap = None
                sc.replay(l2, l1)
                sc.replay(l3, l4)
            release(2)
            def fC(i, par):
                return OFF_FC + (par * 8 + i) * TBW

            def bC(i, par):
                return OFF_C + (par * 7 + i) * TBW

            N64 = TBW // 64
            for par in range(2):
                for (r0, r1, reg) in ((64, 128, 3), (0, 64, 6)):
                    sc.op("dve", lambda e, par=par, r0=r0, r1=r1, reg=reg: e.memset(arB[r0:r1, bC(reg, par): bC(reg, par) + TBW].ap, 0.0), [],
                          [arB[r0:r1, bC(reg, par): bC(reg, par) + TBW]])
            if seg == 0:
                sc.op("dve", lambda e: e.memset(Sst.all().ap, 0.0), [], [Sst.all()])

            def proj(h, tb, wh, pi):
                tok = slice(tb * TBW, (tb + 1) * TBW)
                b1, b2 = banks[(pi % 2) * 2], banks[(pi % 2) * 2 + 1]
                for kc in range(8):
                    MM(b1[:, 0:TBW], W(wh, kc, 0, 128), hT[:, kc, tok], kc == 0, kc == 7)
                for kc in range(8):
                    MM(b1[:, 256:256 + TBW], W(wh, kc, 128, 256), hT[:, kc, tok], kc == 0, kc == 7)
                for kc in range(8):
                    MM(b2[:, 0:TBW], W(wh, kc, 256, 384), hT[:, kc, tok], kc == 0, kc == 7)
                for c in range(NCH):
                    tk = slice(tb * TBW + c * 128, tb * TBW + (c + 1) * 128)
                    for kc in range(8):
                        MM(b2[:, 256 + c * 128: 256 + (c + 1) * 128], hT[:, kc, tk], W(wh, kc, 384, 512), kc == 0, kc == 7)
                return b1, b2

            def scal_views(par):
                so = 232 + par * 24
                er = [small[:, so + 4 + c: so + 5 + c] for c in range(N64)]
                eL = [small[:, so + 8 + c: so + 9 + c] for c in range(N64)]
                aa = [small[:, so + 12 + c: so + 13 + c] for c in range(N64)]
                return so, er, eL, aa

            def prep(h, tb, par, b1, b2):
                omlh = small[:, 64 + 8 * l + h: 65 + 8 * l + h]
                F = lambda i: arF[:, fC(i, par): fC(i, par) + TBW]
                B = lambda i: arB[:, bC(i, par): bC(i, par) + TBW]
                q_sb, kk, sg, logf, cum, E2, lnv = (F(i) for i in range(7))
                v_sb, Qt, Kt = B(0), B(1), B(2)
                zq, zf, zg = b1[:, 0:TBW], b1[:, 256:256 + TBW], b2[:, 0:TBW]
                ACT(kk, zf, AF.Exp)
                ACT(q_sb, zq, AF.Exp, scale=-1.0)
                ACT(sg, zg, AF.Exp, scale=-1.0)
                ACT(kk, kk, AF.Ln, bias=1.0)
                ACT(q_sb, q_sb, AF.Ln, bias=1.0)
                ACT(sg, sg, AF.Ln, bias=1.0)
                ACT(kk, kk, AF.Exp, scale=-1.0)
                ACT(q_sb, q_sb, AF.Exp, scale=-1.0)
                ACT(sg, sg, AF.Exp, scale=-1.0)
                TS_("dve", kk, kk, omlh, None, ALU.mult)
                ACT(logf, kk, AF.Ln, scale=-1.0, bias=1.0)
                TT("dve", q_sb, zq, q_sb, ALU.mult)
                TT("dve", sg, zg, sg, ALU.mult)
                COPY("dve", v_sb, b2[:, 256:256 + TBW])
                so, er, eL, aa = scal_views(par)
                negr = small[:, so: so + N64]
                erv = small[:, so + 4: so + 4 + N64]
                eLv = small[:, so + 8: so + 8 + N64]
                aav = small[:, so + 12: so + 12 + N64]
                lf3 = arF.h[:, fC(3, par): fC(3, par) + TBW].rearrange("p (c t) -> p c t", t=64)[:, :, 0:32]
                sc.op("dve", lambda e: e.tensor_reduce(out=negr.ap, in_=lf3, axis=mybir.AxisListType.X, op=ALU.add, negate=True),
                      [logf], [negr])
                for c in range(N64):
                    cs = slice(fC(4, par) + c * 64, fC(4, par) + (c + 1) * 64)
                    ls = slice(fC(3, par) + c * 64, fC(3, par) + (c + 1) * 64)
                    ini = small[:, so + c: so + c + 1]
                    sc.op("dve", lambda e, cs=cs, ls=ls, ini=ini: e.tensor_tensor_scan(
                        out=arF[:, cs].ap, data0=onesf[:, 0:64].ap, data1=arF[:, ls].ap, initial=ini.ap, op0=ALU.mult, op1=ALU.add),
                        [arF[:, ls], onesf[:, 0:64], ini], [arF[:, cs]])
                ACT(erv, negr, AF.Exp, scale=-1.0)
                dlast = View(arF, arF.h[:, fC(4, par) + 63: fC(4, par) + TBW: 64], fC(4, par), fC(4, par) + TBW)
                ACT(aav, dlast, AF.Exp)
                TT("dve", eLv, aav, erv, ALU.mult)
                ACT(logf, cum, AF.Exp)
                ACT(E2, cum, AF.Exp, scale=-1.0)
                TT("pool", Qt, q_sb, logf, ALU.mult)
                TT("pool", Kt, kk, E2, ALU.mult)

            def mid(h, tb, par, gi, split=None):
                hng = prm[:, pb + 28: pb + 29]
                so, er, eL, aa = scal_views(par)
                F = lambda i: arF[:, fC(i, par): fC(i, par) + TBW]
                B = lambda i: arB[:, bC(i, par): bC(i, par) + TBW]
                sg, lnv = F(2), F(6)
                osq = B(5)

                def Bc(i, c):
                    return arB[:, bC(i, par) + c * 128: bC(i, par) + (c + 1) * 128]

                mA, mB = banks[4], banks[5]
                pk = bbank()
                for c in range(NCH):
                    TR(pk[:, c * 128:(c + 1) * 128], Bc(2, c))
                for c in range(N64):
                    rb = (c % 2) * 64
                    o2, o4 = bC(2, par) + c * 64, bC(1, par) + c * 64
                    MM(mA[rb:rb + 32, c * 64:c * 64 + 32], arB[:, o2:o2 + 32], arB[:, o4:o4 + 32], True, True)
                    MM(mA[rb:rb + 64, c * 64 + 32:c * 64 + 64], arB[:, o2:o2 + 64], arB[:, o4 + 32:o4 + 64], True, True)
                COPY("act", arB[0:64, bC(3, par): bC(3, par) + TBW], pk[0:64, 0:TBW])
                COPY("dve", arB[64:128, bC(6, par): bC(6, par) + TBW], pk[64:128, 0:TBW])
                for c in range(N64):
                    t128 = c // 2
                    kreg = 3 if c % 2 == 0 else 6
                    MM(mB[:, c * 128:(c + 1) * 128], arB[:, bC(kreg, par) + t128 * 128: bC(kreg, par) + (t128 + 1) * 128],
                       arB[:, bC(0, par) + t128 * 128: bC(0, par) + (t128 + 1) * 128], True, True)
                for t128 in range(NCH):
                    TT("dve", Bc(4, t128), mA[:, t128 * 128:(t128 + 1) * 128], maskT, ALU.mult)
                ring = [arF[:, 1152 + k * 128: 1152 + (k + 1) * 128] for k in range(3)]
                tmps = [arF[:, 512 + k * 128: 512 + (k + 1) * 128] for k in range(N64)]
                srcs = [Sst[:, h, :]] + ring
                dsts = ring + [Sst[:, h, :]]
                oo = 256
                Pbs = [arB[:, OFF_PB + c * 128: OFF_PB + (c + 1) * 128] for c in range(N64)]
                for c in range(N64):
                    ACT(tmps[c], mB[:, c * 128:(c + 1) * 128], AF.Identity, scale=aa[c])
                for c in range(N64):
                    TS_("dve", Pbs[c], srcs[c], er[c], None, ALU.mult)
                    STT(dsts[c], srcs[c], eL[c], tmps[c], ALU.mult, ALU.add)
                if split is not None:
                    split.append(len(sc.cap))
                for c in range(N64):
                    t128 = c // 2
                    if c % 2 == 0:
                        MM(mA[:, oo + t128 * 128: oo + (t128 + 1) * 128], Bc(0, t128), Bc(4, t128), True, False)
                    MM(mA[:, oo + c * 64: oo + (c + 1) * 64], Pbs[c], arB[:, bC(1, par) + c * 64: bC(1, par) + (c + 1) * 64], False, c % 2 == 1)
                ACT(osq, mA[:, oo:oo + TBW], AF.Square)
                STT(F(7), mA[:, oo:oo + TBW], hng, sg, ALU.mult, ALU.mult)

            def norm(h, tb, par):
                F = lambda i: arF[:, fC(i, par): fC(i, par) + TBW]
                lnv, ogp = F(6), F(7)
                osq = arB[:, bC(5, par): bC(5, par) + TBW]
                mB = banks[5]
                la_ = []
                sc.cap = la_
                MM(mB[:, 0:TBW], ones_b, osq, True, True)
                ACT(lnv, mB[:, 0:TBW], AF.Ln, scale=1.0 / 128, bias=EPS)
                ACT(lnv, lnv, AF.Exp, scale=-0.5)
                sc.cap = None
                lb_ = []
                sc.cap = lb_
                TT("dve", Og[:, h, tb * TBW:(tb + 1) * TBW], ogp, lnv, ALU.mult)
                sc.cap = None
                return la_, lb_

            items = [(h, tb) for h in range(NH) for tb in range(NTB)]
            nI = len(items)
            pbanks = {}
            whs = {}
            for i in range(nI + 3):
                la, lb, lc = [], [], []
                rel = False
                if i < nI:
                    h, tb = items[i]
                    if tb == 0:
                        if h == NH // 2 and seg == 0 and l + 1 < L:
                            ada(l + 1, banks[5])
                        whs[h] = next_unit(l * UNITS_PER_LAYER + U_HEAD + h)
                    sc.cap = lc
                    pbanks[i] = proj(h, tb, whs[h], i)
                    sc.cap = None
                    rel = tb == NTB - 1
                if 0 <= i - 1 < nI:
                    h1, tb1 = items[i - 1]
                    sc.cap = la
                    prep(h1, tb1, (i - 1) % 2, *pbanks.pop(i - 1))
                    sc.cap = None
                split = []
                if 0 <= i - 2 < nI:
                    h2, tb2 = items[i - 2]
                    sc.cap = lb
                    mid(h2, tb2, (i - 2) % 2, (seg * nI + i - 2), split)
                    sc.cap = None
                n1, n2 = [], []
                if 0 <= i - 3 < nI:
                    h3, tb3 = items[i - 3]
                    n1, n2 = norm(h3, tb3, (i - 3) % 2)
                k = split[0] if split else 0
                sc.replay(n1)
                sc.replay(lb[:k])
                sc.replay(n2)
                sc.replay(lc, la)
                sc.replay(lb[k:])
                if rel:
                    release()

            hoist = nxt_seg is not None and nxt_seg != seg
            if hoist:
                prologue(nxt_seg)
            for j in range(8):
                wd = next_unit(l * UNITS_PER_LAYER + U_D + j)
                for tb in range(TS // 512 if TS >= 512 else 1):
                    wdt = min(512, TS)
                    tok = slice(tb * wdt, (tb + 1) * wdt)
                    g1, g2, ba, bb = bank(), bank(), bank(), bank()
                    for kc in range(8):
                        MM(g1[:, 0:wdt], wd[:, kc * 128:(kc + 1) * 128], hT[:, kc, tok], kc == 0, kc == 7)
                    for kc in range(8):
                        MM(g2[:, 0:wdt], wd[:, 1024 + kc * 128: 1024 + (kc + 1) * 128], hT[:, kc, tok], kc == 0, kc == 7)
                    for kc in range(4):
                        MM(ba[:, 0:wdt], wd[:, 3072 + kc * 128: 3072 + (kc + 1) * 128], Ain[:, kc, tok], kc == 0, kc == 3)
                    for kc in range(8):
                        MM(bb[:, 0:wdt], wd[:, 2048 + kc * 128: 2048 + (kc + 1) * 128], Og[:, kc, tok], kc == 0, kc == 7)
                    par = tb % 2
                    sp_ = arF[:, par * 1024: par * 1024 + wdt]
                    sh_ = arF[:, par * 1024 + 512: par * 1024 + 512 + wdt]
                    ACT(sp_, g1[:, 0:wdt], AF.Sigmoid)
                    ACT(sh_, g2[:, 0:wdt], AF.Sigmoid)
                    TT("dve", sp_, ba[:, 0:wdt], sp_, ALU.mult)
                    TT("dve", sh_, bb[:, 0:wdt], sh_, ALU.mult)
                    TT("dve", mer[:, j, tok], sp_, sh_, ALU.add)
                release()

            wo0 = next_unit(l * UNITS_PER_LAYER + U_OUT)
            wo1 = next_unit(l * UNITS_PER_LAYER + U_OUT + 1)
            for T in range(NT):
                gt = t0 + T
                tok = slice(T * 128, (T + 1) * 128)
                y0, y1 = bank(), bank()
                for kc in range(8):
                    MM(y0[:, 0:512], mer[:, kc, tok], W(wo0, kc, 0, 512), kc == 0, kc == 7)
                for kc in range(8):
                    MM(y1[:, 0:512], mer[:, kc, tok], W(wo1, kc, 0, 512), kc == 0, kc == 7)
                so = 200 + (T % 2) * 4
                sa, sb_, rs = small[:, so:so + 1], small[:, so + 1:so + 2], small[:, so + 2:so + 3]
                ACT(arB[:, OFF_JUNK: OFF_JUNK + 512], y0[:, 0:512], AF.Square, accum=sa)
                ACT(arB[:, OFF_JUNK + 512: OFF_JUNK + 1024], y1[:, 0:512], AF.Square, accum=sb_)
                TT("dve", rs, sa, sb_, ALU.add)
                ACT(rs, rs, AF.Sqrt, scale=1.0 / D, bias=EPS)
                sc.op("dve", lambda e, rs=rs: e.reciprocal(out=rs.ap, in_=rs.ap), [rs], [rs])
                tmp = arF[:, (T % 2) * 1024: (T % 2 + 1) * 1024]
                STT(arF[:, (T % 2) * 1024: (T % 2) * 1024 + 512], y0[:, 0:512], rs, ggt[:, 0:512], ALU.mult, ALU.mult)
                STT(arF[:, (T % 2) * 1024 + 512: (T % 2 + 1) * 1024], y1[:, 0:512], rs, ggt[:, 512:1024], ALU.mult, ALU.mult)
                TT("pool", x[:, gt, :], x[:, gt, :], tmp, ALU.add)
                if last:
                    sc.dma("sp", f"o{gt % 4}", OUTd[gt * 128:(gt + 1) * 128, :], x[:, gt, :])
            release(2)
            if nxt_seg is not None and not hoist:
                prologue(nxt_seg)

        ada(0)
        order = [(l, seg) for l in range(nlayers) for seg in range(NSEG)]
        prologue(0)
        for k, (l, seg) in enumerate(order):
            if seg == 0:
                ada_bcast(l)
            layer_seg(l, seg, last=(l == nlayers - 1), nxt_seg=order[k + 1][1] if k + 1 < len(order) else None)
        sc.wait_all("sp", [f"o{i}" for i in range(4)])
        sc.emit(st)
    return nc


def _pool_mats():
    pm = np.zeros((12, 128, 128), np.float32)
    s = np.arange(128)[:, None]
    t = np.arange(128)[None, :]
    for g, w in enumerate((2, 4, 8, 16)):
        eye = (s == t).astype(np.float32)
        pm[g] = ((s <= t) & (s > t - w)) / w - eye
        pm[4 + g] = ((s - 128) > (t - w)) / w
        cnt = np.minimum(t + 1, w)
        pm[8 + g] = ((s <= t) & (s > t - w)) / cnt - eye
    return pm


def _consts():
    cbm = np.zeros((128, 1920), np.float32)
    cbm[:, 0:128] = np.eye(128)
    s = np.arange(128)[:, None]
    t = np.arange(128)[None, :]
    cbm[:, 128:256] = (s <= t) & ((s // 64) == (t // 64))
    cbm[:, 256:384] = 1.0
    pm = _pool_mats()
    for i in range(12):
        cbm[:, 384 + i * 128: 384 + (i + 1) * 128] = pm[i]
    return cbm.astype(ml_dtypes.bfloat16)


def _fm(v):
    return np.ascontiguousarray(v.reshape(8, 128).T)


def _kunit(wcols):
    n = wcols.shape[1]
    return wcols.reshape(-1, 128, n).transpose(1, 0, 2).reshape(128, -1)


def _pack_weights(w_ada, w_in, w_pool_o, w_hgrn_o, w_out):
    wu = np.zeros((L * UNITS_PER_LAYER, 128, 4096), np.float32)
    for l in range(L):
        b = l * UNITS_PER_LAYER
        for i in range(6):
            wu[b + U_ADA + i] = _kunit(w_ada[l][:, i * 512:(i + 1) * 512])
        wi = w_in[l]
        wu[b + U_PV] = _kunit(wi[:, 0:512])
        wu[b + U_PG] = _kunit(wi[:, 512:1024])
        for h in range(NH):
            cols = np.concatenate([wi[:, 1024 + h * 128: 1024 + (h + 1) * 128],
                                   wi[:, 2048 + h * 128: 2048 + (h + 1) * 128],
                                   wi[:, 4096 + h * 128: 4096 + (h + 1) * 128],
                                   wi[:, 3072 + h * 128: 3072 + (h + 1) * 128]], 1)
            wu[b + U_HEAD + h] = _kunit(cols)
        for j in range(8):
            js = slice(j * 128, (j + 1) * 128)
            wu[b + U_D + j, :, 0:1024] = _kunit(wi[:, 5120:6144][:, js])
            wu[b + U_D + j, :, 1024:2048] = _kunit(wi[:, 6144:7168][:, js])
            wu[b + U_D + j, :, 2048:3072] = _kunit(w_hgrn_o[l][:, js])
            wu[b + U_D + j, :, 3072:3584] = _kunit(w_pool_o[l][:, js])
        wu[b + U_OUT] = _kunit(w_out[l][:, 0:512])
        wu[b + U_OUT + 1] = _kunit(w_out[l][:, 512:1024])
    return wu


def _pack_params(c_row, b_ada, g_pre, g_post, pool_scale, lb_logits, hgrn_norm_g, pool_w):
    prm = np.zeros((128, NPRM), np.float32)
    prm[:, 0:8] = _fm(c_row)
    pwm = np.zeros((128, L * 512), np.float32)
    for l in range(L):
        pb = 8 + l * PRM_L
        prm[:, pb:pb + 8] = _fm(g_pre[l])
        prm[:, pb + 8:pb + 16] = _fm(b_ada[l][0:1024])
        prm[:, pb + 16:pb + 24] = _fm(b_ada[l][1024:2048])
        prm[:, pb + 24:pb + 28] = pool_scale[l].reshape(4, 128).T
        prm[:, pb + 28] = hgrn_norm_g[l] * np.float32(1.0)
        prm[:, pb + 29:pb + 37] = _fm(lb_logits[l])
        prm[:, pb + 37:pb + 45] = _fm(b_ada[l][2048:3072])
        prm[:, pb + 45:pb + 53] = _fm(g_post[l])
        pwm[:, l * 512:(l + 1) * 512] = pool_w[l].transpose(1, 0, 2).reshape(128, 512)
    return prm, pwm


_CACHE = {}


def make_in_maps(x, c, w_ada, b_ada, g_pre, g_post, w_in, pool_w, pool_scale, lb_logits,
                 hgrn_norm_g, w_pool_o, w_hgrn_o, w_out):
    f = lambda a: np.asarray(a, dtype=np.float32)
    x, c, w_ada, b_ada, g_pre, g_post, w_in, pool_w, pool_scale, lb_logits, hgrn_norm_g, w_pool_o, w_hgrn_o, w_out = map(
        f, (x, c, w_ada, b_ada, g_pre, g_post, w_in, pool_w, pool_scale, lb_logits, hgrn_norm_g, w_pool_o, w_hgrn_o, w_out))
    wu = _pack_weights(w_ada, w_in, w_pool_o, w_hgrn_o, w_out)
    cbm = _consts()
    maps = []
    for b in range(x.shape[0]):
        prm, pwm = _pack_params(c[b], b_ada, g_pre, g_post, pool_scale, lb_logits, hgrn_norm_g, pool_w)
        maps.append({"x": np.ascontiguousarray(x[b]), "wu": wu, "prm": prm, "cb": cbm, "pw": pwm})
    return maps


def kernel(**inputs):
    x = np.asarray(inputs["x"])
    B, S, _ = x.shape
    maps = make_in_maps(**inputs)
    if S not in _CACHE:
        _CACHE[S] = build(S)
    res = run_bass_kernel_spmd(_CACHE[S], maps, core_ids=list(range(B)))
    return np.stack([np.asarray(r["out"], dtype=np.float32) for r in res.results], 0)
```
